# Optimizing a Trainium2 kernel written in Bass

```python
import math
import jax, jax.numpy as jnp
from jax import lax
import numpy as np

D_MODEL = 2048
BATCH = 1
SEQ = 8192
DEPTH = 1

GMLP_CHUNK = 128
GMLP_GROUPS = 4
GMLP_GROUP_DIM = 128
GMLP_WIDTH = GMLP_GROUPS * GMLP_GROUP_DIM

DN_HEADS = 8
DN_HEAD_DIM = 128
DN_WIDTH = DN_HEADS * DN_HEAD_DIM
DN_CONV = 4
DN_CHUNK = 64

XA_HEADS = 4
XA_HEAD_DIM = 128
XA_WIDTH = XA_HEADS * XA_HEAD_DIM
N_MEM = 256

MIX_WIDTH = GMLP_WIDTH + DN_WIDTH + XA_WIDTH
EPS = 1e-6

SEG_WIDTHS = (GMLP_WIDTH, GMLP_WIDTH, GMLP_WIDTH,
              DN_WIDTH, DN_WIDTH, DN_WIDTH, DN_WIDTH, DN_HEADS, DN_HEADS,
              XA_WIDTH, XA_WIDTH)
IN_WIDTH = sum(SEG_WIDTHS)
SPLIT_POINTS = tuple(sum(SEG_WIDTHS[:i + 1]) for i in range(len(SEG_WIDTHS) - 1))

kernel_name = "hybrid_gmlp_gated_deltanet_memxattn_block"


def rmsnorm(x, g):
    xf = x.astype(jnp.float32)
    y = xf * lax.rsqrt(jnp.mean(xf * xf, axis=-1, keepdims=True) + EPS)
    return (y * g.astype(jnp.float32)).astype(x.dtype)


def layernorm(x, g, b):
    xf = x.astype(jnp.float32)
    mu = jnp.mean(xf, axis=-1, keepdims=True)
    xc = xf - mu
    y = xc * lax.rsqrt(jnp.mean(xc * xc, axis=-1, keepdims=True) + EPS)
    return (y * g.astype(jnp.float32) + b.astype(jnp.float32)).astype(x.dtype)


def l2norm(x):
    xf = x.astype(jnp.float32)
    return xf * lax.rsqrt(jnp.sum(xf * xf, axis=-1, keepdims=True) + EPS)


def gmlp_spatial_gating(u, v, ws, bs, ln_g, ln_b):
    B, S, _ = u.shape
    u = jax.nn.gelu(u)
    v = layernorm(jax.nn.gelu(v), ln_g, ln_b)
    nc = S // GMLP_CHUNK
    v = v.reshape(B, nc, GMLP_CHUNK, GMLP_GROUPS, GMLP_GROUP_DIM)
    causal = jnp.tril(jnp.ones((GMLP_CHUNK, GMLP_CHUNK), dtype=bool))
    w = jnp.where(causal[None], ws, 0)
    s = jnp.einsum('gts,bcsgd->bctgd', w, v) + bs.T[None, None, :, :, None]
    return u * s.reshape(B, S, GMLP_WIDTH)


def causal_conv_silu(x, w):
    K = w.shape[0]
    S = x.shape[1]
    xp = jnp.pad(x, ((0, 0), (K - 1, 0), (0, 0)))
    y = sum(w[k] * xp[:, k:k + S] for k in range(K))
    return jax.nn.silu(y)


def gated_delta_rule(q, k, v, g, beta):
    f32 = jnp.float32
    q, k, v, g, beta = (t.astype(f32) for t in (q, k, v, g, beta))
    B, H, S, Dk = q.shape
    Dv = v.shape[-1]
    C = DN_CHUNK
    N = S // C
    q = q.reshape(B, H, N, C, Dk)
    k = k.reshape(B, H, N, C, Dk)
    v = v.reshape(B, H, N, C, Dv)
    g = g.reshape(B, H, N, C)
    beta = beta.reshape(B, H, N, C)

    g_cum = jnp.cumsum(g, axis=-1)
    tri = jnp.tril(jnp.ones((C, C), dtype=bool))
    strict = jnp.tril(jnp.ones((C, C), dtype=bool), k=-1)
    diff = g_cum[..., :, None] - g_cum[..., None, :]
    decay = jnp.exp(jnp.where(tri, diff, -jnp.inf))

    kb = k * beta[..., None]
    vb = v * beta[..., None]
    a = jnp.where(strict, jnp.einsum('bhnid,bhnjd->bhnij', kb, k) * decay, 0.0)
    eye = jnp.eye(C, dtype=f32)
    T = lax.linalg.triangular_solve(a + eye, jnp.broadcast_to(eye, a.shape),
                                    left_side=True, lower=True, unit_diagonal=True)
    value = jnp.einsum('bhnij,bhnjd->bhnid', T, vb)
    k_cumdecay = jnp.einsum('bhnij,bhnjd->bhnid', T, kb * jnp.exp(g_cum)[..., None])
    attn_intra = jnp.where(tri, jnp.einsum('bhnid,bhnjd->bhnij', q, k) * decay, 0.0)
    q_g = q * jnp.exp(g_cum)[..., None]
    g_last = g_cum[..., -1]
    k_dec = k * jnp.exp(g_last[..., None] - g_cum)[..., None]

    def step(state, inp):
        qg_c, kcd_c, val_c, ai_c, kd_c, gl_c = inp
        v_new = val_c - jnp.einsum('bhid,bhde->bhie', kcd_c, state)
        o_c = jnp.einsum('bhid,bhde->bhie', qg_c, state) + jnp.einsum('bhij,bhje->bhie', ai_c, v_new)
        state = state * jnp.exp(gl_c)[..., None, None] + jnp.einsum('bhid,bhie->bhde', kd_c, v_new)
        return state, o_c

    xs = tuple(jnp.moveaxis(t, 2, 0) for t in (q_g, k_cumdecay, value, attn_intra, k_dec, g_last))
    s0 = jnp.zeros((B, H, Dk, Dv), f32)
    _, o = lax.scan(step, s0, xs)
    return jnp.moveaxis(o, 0, 2).reshape(B, H, S, Dv)


def setup_inputs(seed: int = 0) -> dict:
    key = jax.random.key(seed)
    ks = jax.random.split(key, 16)
    f32 = jnp.float32
    nrm = lambda k_, shp: jax.random.normal(k_, shp, f32)
    x = nrm(ks[0], (BATCH, SEQ, D_MODEL))
    mem = nrm(ks[1], (BATCH, N_MEM, D_MODEL))
    ln_g = 1.0 + 0.1 * nrm(ks[2], (DEPTH, D_MODEL))
    w_in = nrm(ks[3], (DEPTH, D_MODEL, IN_WIDTH)) * D_MODEL ** -0.5
    gmlp_ln_g = 1.0 + 0.1 * nrm(ks[4], (DEPTH, GMLP_WIDTH))
    gmlp_ln_b = 0.1 * nrm(ks[5], (DEPTH, GMLP_WIDTH))
    gmlp_ws = nrm(ks[6], (DEPTH, GMLP_GROUPS, GMLP_CHUNK, GMLP_CHUNK)) * GMLP_CHUNK ** -0.5
    gmlp_bs = 1.0 + 0.1 * nrm(ks[7], (DEPTH, GMLP_GROUPS, GMLP_CHUNK))
    conv_w = nrm(ks[8], (DEPTH, DN_CONV, 3 * DN_WIDTH)) * DN_CONV ** -0.5
    dn_a_log = jnp.log(jax.random.uniform(ks[9], (DEPTH, DN_HEADS), f32, minval=1.0, maxval=16.0))
    dt = jnp.exp(jax.random.uniform(ks[10], (DEPTH, DN_HEADS), f32,
                                    minval=math.log(1e-3), maxval=math.log(1e-1)))
    dn_dt_bias = dt + jnp.log(-jnp.expm1(-dt))
    dn_norm_g = 1.0 + 0.1 * nrm(ks[11], (DEPTH, DN_HEAD_DIM))
    mem_norm_g = 1.0 + 0.1 * nrm(ks[12], (DEPTH, D_MODEL))
    w_mem_kv = nrm(ks[13], (DEPTH, D_MODEL, 2 * XA_WIDTH)) * D_MODEL ** -0.5
    w_out = nrm(ks[14], (DEPTH, MIX_WIDTH, D_MODEL)) * MIX_WIDTH ** -0.5
    final_g = 1.0 + 0.1 * nrm(ks[15], (D_MODEL,))
    return {"x": x, "mem": mem, "ln_g": ln_g, "w_in": w_in,
            "gmlp_ln_g": gmlp_ln_g, "gmlp_ln_b": gmlp_ln_b, "gmlp_ws": gmlp_ws, "gmlp_bs": gmlp_bs,
            "conv_w": conv_w, "dn_a_log": dn_a_log, "dn_dt_bias": dn_dt_bias, "dn_norm_g": dn_norm_g,
            "mem_norm_g": mem_norm_g, "w_mem_kv": w_mem_kv, "w_out": w_out, "final_g": final_g}


def reference(x, mem, ln_g, w_in, gmlp_ln_g, gmlp_ln_b, gmlp_ws, gmlp_bs, conv_w, dn_a_log,
              dn_dt_bias, dn_norm_g, mem_norm_g, w_mem_kv, w_out, final_g):
    B, S, _ = x.shape
    M = mem.shape[1]
    for l in range(DEPTH):
        h = rmsnorm(x, ln_g[l])
        proj = h @ w_in[l]
        (g_u, g_v, g_z, d_q, d_k, d_v, d_z, d_a, d_b, c_q, c_z) = jnp.split(proj, SPLIT_POINTS, axis=-1)

        out_a = gmlp_spatial_gating(g_u, g_v, gmlp_ws[l], gmlp_bs[l], gmlp_ln_g[l], gmlp_ln_b[l]) * jax.nn.silu(g_z)

        qkv = causal_conv_silu(jnp.concatenate([d_q, d_k, d_v], axis=-1), conv_w[l])
        q, k, v = jnp.split(qkv, 3, axis=-1)
        to_heads = lambda t: t.reshape(B, S, DN_HEADS, DN_HEAD_DIM).transpose(0, 2, 1, 3)
        q = l2norm(to_heads(q)) * DN_HEAD_DIM ** -0.5
        k = l2norm(to_heads(k))
        v = to_heads(v)
        g = -jnp.exp(dn_a_log[l].astype(jnp.float32)) * jax.nn.softplus(
            d_a.astype(jnp.float32) + dn_dt_bias[l].astype(jnp.float32))
        beta = jax.nn.sigmoid(d_b.astype(jnp.float32))
        o = gated_delta_rule(q, k, v, g.transpose(0, 2, 1), beta.transpose(0, 2, 1))
        o = o.transpose(0, 2, 1, 3).astype(x.dtype)
        o = rmsnorm(o, dn_norm_g[l]) * jax.nn.silu(d_z.reshape(B, S, DN_HEADS, DN_HEAD_DIM))
        out_b = o.reshape(B, S, DN_WIDTH)

        m = rmsnorm(mem, mem_norm_g[l])
        mk, mv = jnp.split(m @ w_mem_kv[l], 2, axis=-1)
        mk = mk.reshape(B, M, XA_HEADS, XA_HEAD_DIM)
        mv = mv.reshape(B, M, XA_HEADS, XA_HEAD_DIM)
        cq = c_q.reshape(B, S, XA_HEADS, XA_HEAD_DIM)
        scores = jnp.einsum('bshd,bmhd->bhsm', cq, mk).astype(jnp.float32) * XA_HEAD_DIM ** -0.5
        p = jax.nn.softmax(scores, axis=-1).astype(mv.dtype)
        out_c = jnp.einsum('bhsm,bmhd->bshd', p, mv).reshape(B, S, XA_WIDTH) * jax.nn.silu(c_z)

        mixed = jnp.concatenate([out_a, out_b, out_c], axis=-1)
        x = x + mixed @ w_out[l]
    return rmsnorm(x, final_g)
```

```python
import contextlib
import numpy as np
import concourse.bass as bass
import concourse.mybir as mybir
from concourse.bass_utils import run_bass_kernel_spmd

F32 = mybir.dt.float32; BF16 = mybir.dt.bfloat16; I32 = mybir.dt.int32
ALU = mybir.AluOpType; AF = mybir.ActivationFunctionType; AX = mybir.AxisListType

S = 8192; D = 2048; NCORE = 8; TS = 1024; NT = 8
EPS = 1e-6
NEG = -30000.0
DEBUG = False
import os as _os
SCHED = _os.environ.get('SCHED') is not None


class Prog:
    uid = 0

    def __init__(self, nc):
        self.nc = nc; self.ops = []; self.lastw = {}; self.readers = {}; self.closed = False

    def add(self, eng, fn, r=(), w=(), dma=None, force=False, n=None, f32=False):
        if self.closed: return None
        if dma is not None:
            cost = 2500.0 + (n or 0) * 0.5
        elif eng == 'pe':
            cost = (n or 128) * (4.0 if f32 else 1.0) / 2.4 + 25.0
        elif eng == 'pool':
            cost = 300.0 + (n or 128) * 1.6
        else:
            cost = 220.0 + (n or 128) * 1.05
        deps = set()
        for x in r:
            if x in self.lastw: deps.add(self.lastw[x])
        for x in w:
            if x in self.lastw: deps.add(self.lastw[x])
            deps.update(self.readers.get(x, ()))
        idx = len(self.ops)
        import inspect
        self.ops.append(dict(eng=eng, fn=fn, deps=deps, dma=dma, force=force, cost=cost, line=inspect.currentframe().f_back.f_lineno))
        for x in w:
            self.lastw[x] = idx; self.readers[x] = []
        for x in r:
            if x not in w: self.readers.setdefault(x, []).append(idx)
        return idx

    def stage(self, name):
        import os
        if os.environ.get('BISECT') == name: self.closed = True

    def schedule(self):
        ops = self.ops; n = len(ops)
        succ = [[] for _ in range(n)]; indeg = [0] * n
        for i, o in enumerate(ops):
            for d in o['deps']:
                succ[d].append(i); indeg[i] += 1
        bl = [0.0] * n
        for i in range(n - 1, -1, -1):
            m = 0.0
            for j in succ[i]:
                if bl[j] > m: m = bl[j]
            bl[i] = m + ops[i]['cost']
        LAT = 250.0
        tfree = {}; rdy = [0.0] * n; fin = [0.0] * n
        ready = {}
        for i in range(n):
            if indeg[i] == 0: ready.setdefault(ops[i]['eng'], []).append(i)
        order = []
        while len(order) < n:
            best = None
            for e, lst in ready.items():
                if not lst: continue
                te = tfree.get(e, 0.0)
                cand = None; ck = None
                for i in lst:
                    st = rdy[i] if rdy[i] > te else te
                    k = (st, -bl[i], i)
                    if ck is None or k < ck: ck = k; cand = i
                if best is None or ck < best[0]: best = (ck, e, cand)
            (st, _, _), e, i = best
            ready[e].remove(i)
            o = ops[i]
            if o['dma'] is not None:
                tfree[e] = st + (900.0 if e == 'pool' else 120.0); fin[i] = st + o['cost']
            else:
                tfree[e] = st + o['cost']; fin[i] = st + o['cost']
            order.append(i)
            for j in succ[i]:
                t = fin[i] + LAT
                if t > rdy[j]: rdy[j] = t
                indeg[j] -= 1
                if indeg[j] == 0: ready.setdefault(ops[j]['eng'], []).append(j)
        self.est_ns = max(fin) if fin else 0.0
        remap = {old: new for new, old in enumerate(order)}
        newops = []
        for old in order:
            o = ops[old]; o['deps'] = {remap[d] for d in o['deps']}; newops.append(o)
        self.ops = newops

    def emit(self, stack, semstack, extra_sems=None, sched=True):
        if sched and SCHED: self.schedule()
        nc = self.nc; ops = self.ops
        if _os.environ.get('DUMP'):
            with open(_os.environ['DUMP'], 'a') as f:
                for i, o in enumerate(ops): f.write('%d %s L%d deps=%s\n' % (i, o['eng'], o['line'], sorted(o['deps'])))
                f.write('----\n')
        needed = [o['force'] or o['dma'] is not None for o in ops]
        for o in ops:
            for d in o['deps']:
                p = ops[d]
                if p['dma'] is None and p['eng'] == 'pe' and o['eng'] == 'pe' and o['dma'] is None:
                    continue
                needed[d] = True
        engs = ['pe', 'act', 'dve', 'pool', 'sp']
        Prog.uid += 1
        esem = {e: semstack.enter_context(nc.semaphore("s%d_%s" % (Prog.uid, e))) for e in engs}
        dnames = sorted({o['dma'] for o in ops if o['dma'] is not None})
        dsem = {}
        for n in dnames:
            if extra_sems and n in extra_sems: dsem[n] = extra_sems[n]
            else: dsem[n] = semstack.enter_context(nc.semaphore("d%d_%s" % (Prog.uid, n)))
        ecnt = {e: 0 for e in engs}; dcnt = {n: 0 for n in dnames}
        for i, o in enumerate(ops):
            o['sig'] = None
            if not needed[i]: continue
            if o['dma'] is not None:
                inc = 1 if o['dma'].startswith('cc') else 16
                dcnt[o['dma']] += inc; o['sig'] = (dsem[o['dma']], dcnt[o['dma']], inc)
            else:
                ecnt[o['eng']] += 1; o['sig'] = (esem[o['eng']], ecnt[o['eng']], 1)
        block = stack.enter_context(nc.Block())

        def run(ename, e):
            waited = {}
            for i, o in enumerate(ops):
                if o['eng'] != ename: continue
                for d in sorted(o['deps']):
                    p = ops[d]
                    if p['sig'] is None: continue
                    sem, val, _ = p['sig']
                    k = id(sem)
                    if waited.get(k, 0) >= val: continue
                    e.wait_ge(sem, val); waited[k] = val
                ins = o['fn'](e)
                if o['sig'] is not None:
                    ins.then_inc(o['sig'][0], o['sig'][2])
        block.tensor(lambda e: run('pe', e))
        block.scalar(lambda e: run('act', e))
        block.vector(lambda e: run('dve', e))
        block.gpsimd(lambda e: run('pool', e))
        block.sync(lambda e: run('sp', e))


class Rec:
    def __init__(self): self.items = []
    def add(self, *a, **k): self.items.append((a, k))
    def stage(self, n): pass


def merge_into(P, recs):
    recs = [r for r in recs if r.items]
    pos = [0] * len(recs)
    while True:
        best = None
        for i, r in enumerate(recs):
            if pos[i] >= len(r.items): continue
            frac = pos[i] / len(r.items)
            if best is None or frac < best[0]: best = (frac, i)
        if best is None: break
        i = best[1]
        a_, k_ = recs[i].items[pos[i]]; pos[i] += 1
        P.add(*a_, **k_)


def build_nc(phases=('A2', 'A1', 'B'), nblk=S // 512, DEBUG=False):
    nc = bass.Bass("TRN2", target_bir_lowering=False)
    dt_in = lambda n, shp, dt=F32: nc.dram_tensor(n, shp, dt, kind="ExternalInput").ap()
    xT = dt_in("xT", [D, S]); wdn = dt_in("wdn", [D, 417]); cw = dt_in("cw", [128, 12])
    hp = dt_in("hp", [128, 2]); xtok = dt_in("xtok", [TS, D]); xTm = dt_in("xTm", [D, TS])
    cid = dt_in("cid", [1, 1], I32); wtok = dt_in("wtok", [D, 3584]); lng = dt_in("lng", [128, 16])
    glg = dt_in("glg", [128, 512]); glb = dt_in("glb", [128, 512]); wsT = dt_in("wsT", [128, 512])
    bsT = dt_in("bsT", [128, 4]); dng = dt_in("dng", [128, 128]); memT = dt_in("memT", [D, 256])
    memd = dt_in("mem", [256, D]); mng = dt_in("mng", [128, 16]); wkv = dt_in("wkv", [D, 1024])
    wout = dt_in("wout", [D, D]); fgd = dt_in("fg", [128, D])
    y = nc.dram_tensor("y", [TS, D], F32, kind="ExternalOutput").ap()
    ib = nc.dram_tensor("ib", [S, 128], BF16)
    ob = nc.dram_tensor("ob", [NCORE * S, 128], BF16)
    if DEBUG:
        dbg_o = nc.dram_tensor("dbg_o", [S, 128], BF16, kind="ExternalOutput").ap()
        dbg_a = nc.dram_tensor("dbg_a", [128, 8 * 512], BF16, kind="ExternalOutput").ap()
        dbg_c = nc.dram_tensor("dbg_c", [128, 8 * 512], BF16, kind="ExternalOutput").ap()

    with contextlib.ExitStack() as top:
        TT = lambda st, name, shp, dt: st.enter_context(nc.sbuf_tensor(name, shp, dt))
        out_a = TT(top, "out_a", [128, NT, 512], BF16); out_c = TT(top, "out_c", [128, NT, 512], BF16)
        sdz = TT(top, "sdz", [128, NT, 1024], BF16)
        pre_sems = {n: top.enter_context(nc.semaphore("pre_" + n)) for n in ('xg',)}
        idf = TT(top, "idf", [128, 128], F32); idb = TT(top, "idb", [128, 128], BF16)
        onesf = TT(top, "onesf", [128, 128], F32); onesb = TT(top, "onesb", [128, 128], BF16)
        epsc = TT(top, "epsc", [128, 1], F32)
        mid = contextlib.ExitStack()
        xg = TT(mid, "xg", [128, 16, TS], BF16)
        cc_sem = top.enter_context(nc.semaphore("cc_sem"))
        reg = top.enter_context(nc.sync.register("cidreg"))
        offv = [None]

        with contextlib.ExitStack() as st:
            P = Prog(nc)
            P.add('pool', lambda e: e.memset(onesf[:], 1.0), w=['onesf'])
            P.add('pool', lambda e: e.memset(idf[:], 1.0), w=['idf'])
            P.add('pool', lambda e: e.affine_select(out=idf[:], in_=idf[:], pattern=[[-1, 128]], compare_op=ALU.is_equal, fill=0.0, base=0, channel_multiplier=1), r=['idf'], w=['idf'])
            P.add('dve', lambda e: e.tensor_copy(out=idb[:], in_=idf[:]), r=['idf'], w=['idb'])
            P.add('dve', lambda e: e.tensor_copy(out=onesb[:], in_=onesf[:]), r=['onesf'], w=['onesb'])
            P.add('dve', lambda e: e.memset(epsc[:], EPS), w=['epsc'])
            def ld_cid(e):
                ins = e.reg_load(reg, cid[0:1, 0:1])
                offv[0] = e.snap(reg, min_val=0, max_val=NCORE - 1)
                return ins
            P.add('sp', ld_cid)
            P.emit(st, top)
        nc.all_engine_barrier()

        with contextlib.ExitStack() as st:
          if 'A2' in phases:
              ps = [st.enter_context(nc.psum_tensor("psA%d" % i, [128, 512], F32)) for i in range(7)]
              pb = [st.enter_context(nc.psum_tensor("pbA%d" % i, [128, 1024], BF16)) for i in range(1)]
              P = Prog(nc)
              T = lambda name, shp, dt: TT(st, name, shp, dt)
              xblk = [T("xblk%d" % i, [128, 16, 512], BF16) for i in range(2)]
              xsq = T("xsq", [128, 16, 512], BF16)
              wdb = T("wdb", [128, 16, 417], BF16)
              lngs = T("lngs", [128, 16], F32); cws = T("cws", [128, 12], F32); hps = T("hps", [128, 2], F32)
              nA = T("nA", [128, 1], F32); dngs = T("dngs", [128, 128], F32)
              U = T("U", [128, 128], F32); nm_s = T("nm_s", [128, 128], F32); nm_i = T("nm_i", [128, 128], F32)
              rstd = T("rstd", [128, 512], F32)
              cin = [T("cin%d" % s, [128, 515], F32) for s in range(3)]; cacc = [T("cacc%d" % s, [128, 512], F32) for s in range(2)]
              abr = T("abr", [64, 512], F32); gT = T("gT", [64, 512], F32); tmpr = T("tmpr", [64, 512], F32)
              raw01 = [T("raw%d" % s, [128, 512], BF16) for s in range(2)]; raw2s = [T("raw2_%d" % s, [128, 512], BF16) for s in range(2)]
              sqb = T("sqb", [128, 512], BF16); rn = T("rn", [128, 512], F32)
              knTs = [T("knT%d" % i, [128, 512], BF16) for i in range(2)]; kqs = [T("kq%d" % i, [128, 4, 2, 128], BF16) for i in range(2)]
              gbs = [T("gb%d" % i, [128, 8], F32) for i in range(2)]; css = [T("cs%d" % i, [128, 8], F32) for i in range(2)]; scs = [T("sc%d" % i, [128, 24], F32) for i in range(2)]
              Sf = T("Sf", [128, 128], F32); Sb = T("Sb", [128, 128], BF16)
              GU = T("GU", [128, 4, 128], F32); NGU = T("NGU", [128, 4, 128], F32)
              Dm = T("Dm", [128, 2, 4, 128], F32); E2 = T("E2", [128, 2, 4, 128], F32); eR = T("eR", [128, 4, 128], F32)
              Xs = [T("X_%d" % j, [128, 4, 128], F32) for j in range(2)]; XTs = [T("XT_%d" % j, [128, 4, 128], F32) for j in range(2)]
              PT = T("PT", [128, 4, 128], F32); TTb = T("TTb", [128, 4, 128], BF16)
              kvt = T("kvt", [128, 4, 2, 128], BF16); kbg = T("kbg", [128, 4, 128], BF16); vb = T("vb", [128, 4, 128], BF16)
              BS = []
              for i in range(2):
                  BS.append(dict(qgT=T("qgT%d" % i, [128, 4, 128], BF16), aiT=T("aiT%d" % i, [128, 4, 128], BF16), kd=T("kd%d" % i, [128, 4, 128], BF16),
                                 kcdT=T("kcdT%d" % i, [128, 4, 128], BF16), val=T("val%d" % i, [128, 4, 128], F32), ost=T("ost%d" % i, [128, 4, 128], BF16)))
              vnews = [T("vnew%d" % i, [128, 128], BF16) for i in range(2)]
              osq = T("osq", [128, 4, 128], F32); ssq = T("ssq", [128, 8], F32)

              P.add('pool', lambda e: e.memset(U[:], 1.0), w=['U'])
              P.add('pool', lambda e: e.affine_select(out=U[:], in_=U[:], pattern=[[1, 128]], compare_op=ALU.is_ge, fill=0.0, base=0, channel_multiplier=-1), r=['U'], w=['U'])
              P.add('pool', lambda e: e.memset(nm_i[:], 0.0), w=['nm_i'])
              P.add('pool', lambda e: e.affine_select(out=nm_i[:], in_=nm_i[:], pattern=[[1, 128]], compare_op=ALU.is_ge, fill=NEG, base=0, channel_multiplier=-1), r=['nm_i'], w=['nm_i'])
              P.add('pool', lambda e: e.memset(nm_s[:], 0.0), w=['nm_s'])
              P.add('pool', lambda e: e.affine_select(out=nm_s[:], in_=nm_s[:], pattern=[[1, 128]], compare_op=ALU.is_gt, fill=NEG, base=0, channel_multiplier=-1), r=['nm_s'], w=['nm_s'])
              P.add('pool', lambda e: e.dma_start(out=wdb[:], in_=wdn.rearrange("(k p) c -> p k c", p=128)), w=['wst'], dma='u1')
              P.add('sp', lambda e: e.dma_start(out=lngs[:], in_=lng), w=['lngs'], dma='u2')
              P.add('sp', lambda e: e.dma_start(out=cws[:], in_=cw), w=['cws'], dma='u3')
              P.add('sp', lambda e: e.dma_start(out=hps[:], in_=hp), w=['hps'], dma='u4')
              P.add('sp', lambda e: e.dma_start(out=dngs[:], in_=dng), w=['dngs'], dma='u5')
              for k in range(16):
                  if k % 2 == 0:
                      P.add('dve', (lambda k: lambda e: e.tensor_scalar(out=wdb[:, k, :], in0=wdb[:, k, :], scalar1=lngs[:, k:k + 1], scalar2=None, op0=ALU.mult))(k),
                            r=['wst', 'lngs'], w=['wdb%d' % k])
                  else:
                      P.add('act', (lambda k: lambda e: e.activation(out=wdb[:, k, :], in_=wdb[:, k, :], func=AF.Copy, scale=lngs[:, k:k + 1]))(k),
                            r=['wst', 'lngs'], w=['wdb%d' % k])
              wdb_all = ['wdb%d' % k for k in range(16)]
              P.add('act', lambda e: e.activation(out=nA[:], in_=hps[:, 0:1], func=AF.Exp), r=['hps'], w=['nA'])
              P.add('dve', lambda e: e.tensor_scalar(out=nA[:], in0=nA[:], scalar1=-1.0, scalar2=None, op0=ALU.mult), r=['nA'], w=['nA'])
              P.add('dve', lambda e: e.memset(Sf[:], 0.0), w=['Sf'])
              P.add('dve', lambda e: e.memset(Sb[:], 0.0), w=['Sb'])
              for s in range(3):
                  P.add('pool', (lambda s: lambda e: e.memset(cin[s][:, 0:3], 0.0))(s), w=['cin%d' % s])

              P.stage('consts')
              chunk_idx = [0]
              def streamA(P, b):
                  sl = b % 2; xb = xblk[sl]; xr = 'xblk%d' % sl
                  sa = b % 2; knT = knTs[sa]; kq = kqs[sa]; gb = gbs[sa]; cs = css[sa]; sc = scs[sa]
                  raw = [raw01[0], raw01[1], raw2s[sa]]
                  rawn = lambda s: ('raw%d' % s) if s < 2 else ('raw2_%d' % sa)
                  P.add('pool', (lambda xb: lambda e: e.tensor_tensor(out=xsq[:, 0:6, :], in0=xb[:, 0:6, :], in1=xb[:, 0:6, :], op=ALU.mult))(xb), r=[xr], w=['xsqa'], n=3072)
                  P.add('dve', (lambda xb: lambda e: e.tensor_tensor(out=xsq[:, 6:16, :], in0=xb[:, 6:16, :], in1=xb[:, 6:16, :], op=ALU.mult))(xb), r=[xr], w=['xsqb'], n=2560)
                  for k in range(16):
                      P.add('pe', (lambda k: lambda e: e.matmul(ps[0][:, :], lhsT=onesb[:], rhs=xsq[:, k, :], start=(k == 0), stop=(k == 15)))(k),
                            r=['onesb', 'xsqa', 'xsqb'], w=['ps0'])
                  P.add('act', lambda e: e.activation(out=rstd[:], in_=ps[0][:, :], func=AF.Ln, scale=1.0 / D, bias=epsc[:, 0:1]), r=['ps0', 'epsc'], w=['rstd'], n=512)
                  P.add('act', lambda e: e.activation(out=rstd[:], in_=rstd[:], func=AF.Exp, scale=-0.5), r=['rstd'], w=['rstd'], n=512)
                  P.stage('rstd')
                  for s in range(3):
                      bk = (1, 2, 1)[s]
                      for k in range(16):
                          P.add('pe', (lambda s, k, xb, bk: lambda e: e.matmul(ps[bk][:, :], lhsT=wdb[:, k, s * 128:(s + 1) * 128], rhs=xb[:, k, :], start=(k == 0), stop=(k == 15)))(s, k, xb, bk),
                                r=[xr, 'wdb%d' % k], w=['ps%d' % bk], n=512)
                      if b > 0:
                          P.add('pool', (lambda s: lambda e: e.tensor_copy(out=cin[s][:, 0:3], in_=cin[s][:, 512:515]))(s), r=['cin%d' % s], w=['cin%d' % s])
                      P.add('dve', (lambda s, bk: lambda e: e.tensor_tensor(out=cin[s][:, 3:515], in0=ps[bk][:, :], in1=rstd[:], op=ALU.mult))(s, bk),
                            r=['ps%d' % bk, 'rstd'], w=['cin%d' % s], n=512)
                  for k in range(16):
                      P.add('pe', (lambda k, xb: lambda e: e.matmul(ps[2][0:33, :], lhsT=wdb[:, k, 384:417], rhs=xb[:, k, :], start=(k == 0), stop=(k == 15)))(k, xb),
                            r=[xr, 'wdb%d' % k], w=['ps2'], n=512)
                  P.add('dve', lambda e: e.tensor_tensor(out=abr[0:33, :], in0=ps[2][0:33, :], in1=rstd[0:33, :], op=ALU.mult), r=['ps2', 'rstd'], w=['abr'], n=512)
                  P.stage('inproj')
                  P.add('act', lambda e: e.activation(out=tmpr[0:1, :], in_=abr[0:1, :], func=AF.Exp, bias=hps[0:1, 1:2], scale=1.0), r=['abr', 'hps'], w=['tmpr'])
                  P.add('act', lambda e: e.activation(out=tmpr[0:1, :], in_=tmpr[0:1, :], func=AF.Ln, bias=1.0), r=['tmpr'], w=['tmpr'])
                  P.add('dve', lambda e: e.tensor_scalar(out=gT[0:1, :], in0=tmpr[0:1, :], scalar1=nA[0:1, 0:1], scalar2=None, op0=ALU.mult), r=['tmpr', 'nA'], w=['gT'])
                  P.add('act', lambda e: e.activation(out=tmpr[32:33, :], in_=abr[32:33, :], func=AF.Exp, scale=-1.0), r=['abr', 'gT'], w=['tmpr'])
                  P.add('dve', lambda e: e.tensor_scalar(out=tmpr[32:33, :], in0=tmpr[32:33, :], scalar1=1.0, scalar2=None, op0=ALU.add), r=['tmpr'], w=['tmpr'])
                  P.add('dve', lambda e: e.reciprocal(out=gT[32:33, :], in_=tmpr[32:33, :]), r=['tmpr'], w=['gT'])
                  P.stage('gbeta')
                  for s in range(3):
                      ca = cacc[s % 2]; can = 'cacc%d' % (s % 2)
                      P.add('act', (lambda s, ca: lambda e: e.activation(out=ca[:], in_=cin[s][:, 0:512], func=AF.Copy, scale=cws[:, s * 4:s * 4 + 1]))(s, ca),
                            r=['cin%d' % s, 'cws'], w=[can], n=512)
                      for k in range(1, 4):
                          P.add('dve', (lambda s, k, ca: lambda e: e.scalar_tensor_tensor(out=ca[:], in0=cin[s][:, k:k + 512], scalar=cws[:, s * 4 + k:s * 4 + k + 1], in1=ca[:], op0=ALU.mult, op1=ALU.add))(s, k, ca),
                                r=['cin%d' % s, 'cws', can], w=[can], n=512)
                      P.add('act', (lambda s, ca: lambda e: e.activation(out=raw[s][:], in_=ca[:], func=AF.Silu))(s, ca), r=[can], w=[rawn(s)], n=512)
                  P.stage('conv')
                  for s in range(2):
                      P.add('pool', (lambda s: lambda e: e.tensor_tensor(out=sqb[:], in0=raw[s][:], in1=raw[s][:], op=ALU.mult))(s), r=[rawn(s)], w=['sqb'])
                      P.add('pe', lambda e: e.matmul(ps[0][:, :], lhsT=onesb[:], rhs=sqb[:], start=True, stop=True), r=['onesb', 'sqb'], w=['ps0'])
                      P.add('act', lambda e: e.activation(out=rn[:], in_=ps[0][:, :], func=AF.Ln, scale=1.0, bias=epsc[:, 0:1]), r=['ps0', 'epsc'], w=['rn'], n=512)
                      P.add('act', lambda e: e.activation(out=rn[:], in_=rn[:], func=AF.Exp, scale=-0.5), r=['rn'], w=['rn'], n=512)
                      if s == 0:
                          P.add('dve', lambda e: e.scalar_tensor_tensor(out=kq[:, :, 1, :], in0=raw[0][:].rearrange("p (c t) -> p c t", c=4), scalar=128.0 ** -0.5,
                                                                        in1=rn[:].rearrange("p (c t) -> p c t", c=4), op0=ALU.mult, op1=ALU.mult), r=['raw0', 'rn'], w=['kq%d' % sa])
                      else:
                          P.add('dve', lambda e: e.tensor_tensor(out=knT[:], in0=raw[1][:], in1=rn[:], op=ALU.mult), r=['raw1', 'rn'], w=['knT%d' % sa])
                  P.stage('l2')
                  P.add('pe', lambda e: e.matmul(ps[2][:, :], lhsT=onesf[32:33, :], rhs=gT[32:33, :], start=True, stop=True), r=['onesf', 'gT'], w=['ps2'], n=512, f32=True)
                  P.add('dve', lambda e: e.tensor_tensor(out=kq[:, :, 0, :], in0=knT[:].rearrange("p (c t) -> p c t", c=4), in1=ps[2][:, :].rearrange("p (c t) -> p c t", c=4), op=ALU.mult),
                        r=['knT%d' % sa, 'ps2'], w=['kq%d' % sa])
                  P.stage('bbc')
                  for c in range(4):
                      P.add('pe', (lambda c: lambda e: e.matmul(ps[0][:, c:c + 1], lhsT=gT[0:1, c * 128:(c + 1) * 128], rhs=onesf[0:1, 0:1], start=True, stop=True))(c),
                            r=['gT', 'onesf'], w=['ps0'])
                      P.add('pe', (lambda c: lambda e: e.matmul(ps[0][:, 4 + c:5 + c], lhsT=gT[32:33, c * 128:(c + 1) * 128], rhs=onesf[32:33, 0:1], start=True, stop=True))(c),
                            r=['gT', 'onesf'], w=['ps0'])
                  P.add('dve', lambda e: e.tensor_copy(out=gb[:], in_=ps[0][:, 0:8]), r=['ps0'], w=['gb%d' % sa])
                  P.add('pe', lambda e: e.matmul(ps[0][:, 8:12], lhsT=U[:], rhs=gb[:, 0:4], start=True, stop=True), r=['U', 'gb%d' % sa], w=['ps0'])
                  P.add('pe', lambda e: e.matmul(ps[0][:, 12:16], lhsT=onesf[:], rhs=gb[:, 0:4], start=True, stop=True), r=['onesf', 'gb%d' % sa], w=['ps0'])
                  P.add('dve', lambda e: e.tensor_copy(out=cs[:], in_=ps[0][:, 8:16]), r=['ps0'], w=['cs%d' % sa])
                  P.add('act', lambda e: e.activation(out=sc[:, 0:8], in_=cs[:, 0:8], func=AF.Exp), r=['cs%d' % sa], w=['sc%d' % sa])
                  P.add('dve', lambda e: e.tensor_tensor(out=sc[:, 16:20], in0=cs[:, 4:8], in1=cs[:, 0:4], op=ALU.subtract), r=['cs%d' % sa, 'sc%d' % sa], w=['sc%d' % sa])
                  P.add('act', lambda e: e.activation(out=sc[:, 8:12], in_=sc[:, 16:20], func=AF.Exp), r=['sc%d' % sa], w=['sc%d' % sa])
                  P.add('dve', lambda e: e.tensor_tensor(out=sc[:, 12:16], in0=sc[:, 0:4], in1=gb[:, 4:8], op=ALU.mult), r=['sc%d' % sa, 'gb%d' % sa], w=['sc%d' % sa])

              def streamB(P, b):
                  sa = b % 2; knT = knTs[sa]; kq = kqs[sa]; gb = gbs[sa]; cs = css[sa]; sc = scs[sa]
                  raw = [raw01[0], raw01[1], raw2s[sa]]
                  bp = b % 2; hs = BS[bp]; hb = 'b%d_' % bp
                  c4 = lambda ap: ap.rearrange("p (c f) -> p c f", c=4)
                  bc_mid = lambda t2: t2[:].unsqueeze(1).broadcast_to([128, 4, 128])
                  bc_last = lambda colap: colap.unsqueeze(2).broadcast_to([128, 4, 128])
                  P.add('pool', lambda e: e.tensor_tensor(out=GU[:], in0=bc_mid(U), in1=bc_last(gb[:, 0:4]), op=ALU.mult), r=['U', 'gb%d' % sa], w=['GU'], n=512)
                  P.add('pe', lambda e: e.matmul(ps[5][:, :], lhsT=onesf[:], rhs=GU[:].rearrange("p c f -> p (c f)"), start=True, stop=True), r=['onesf', 'GU'], w=['ps5'], n=512, f32=True)
                  P.add('dve', lambda e: e.tensor_tensor(out=NGU[:], in0=c4(ps[5][:, :]), in1=bc_last(cs[:, 0:4]), op=ALU.subtract), r=['ps5', 'cs%d' % sa], w=['NGU'], n=512)
                  P.add('pool', lambda e: e.tensor_tensor(out=Dm[:, 0, :, :], in0=NGU[:], in1=bc_mid(nm_s), op=ALU.add), r=['NGU', 'nm_s'], w=['Dm'], n=512)
                  P.add('dve', lambda e: e.tensor_tensor(out=Dm[:, 1, :, :], in0=NGU[:], in1=bc_mid(nm_i), op=ALU.add), r=['NGU', 'nm_i', 'Dm'], w=['Dm'], n=512)
                  P.add('act', lambda e: e.activation(out=E2[:].rearrange("p a c f -> p (a c f)"), in_=Dm[:].rearrange("p a c f -> p (a c f)"), func=AF.Exp), r=['Dm'], w=['E2'], n=1024)
                  P.add('act', lambda e: e.activation(out=eR[:].rearrange("p c f -> p (c f)"), in_=ps[5][:, :], func=AF.Exp), r=['ps5'], w=['eR'], n=512)
                  P.add('dve', (lambda hs: lambda e: e.tensor_tensor(out=hs['qgT'][:], in0=kq[:, :, 1, :], in1=eR[:], op=ALU.mult))(hs), r=['kq%d' % sa, 'eR'], w=[hb + 'qgT'], n=512)
                  P.stage('decay')
                  for c in range(4):
                      P.add('pe', (lambda c: lambda e: e.matmul(ps[6][:, c * 128:(c + 1) * 128], lhsT=knT[:, c * 128:(c + 1) * 128], rhs=kq[:, c, 0, :], start=True, stop=True))(c), r=['knT%d' % sa, 'kq%d' % sa], w=['ps6'])
                      P.add('pe', (lambda c: lambda e: e.matmul(ps[4][:, c * 128:(c + 1) * 128], lhsT=knT[:, c * 128:(c + 1) * 128], rhs=kq[:, c, 1, :], start=True, stop=True))(c), r=['knT%d' % sa, 'kq%d' % sa], w=['ps4'])
                  P.add('dve', lambda e: e.tensor_tensor(out=XTs[0][:], in0=c4(ps[6][:, :]), in1=E2[:, 0, :, :], op=ALU.mult), r=['ps6', 'E2'], w=['XT0'], n=512)
                  P.add('dve', (lambda hs: lambda e: e.tensor_tensor(out=hs['aiT'][:], in0=c4(ps[4][:, :]), in1=E2[:, 1, :, :], op=ALU.mult))(hs), r=['ps4', 'E2'], w=[hb + 'aiT'], n=512)
                  P.add('dve', lambda e: e.scalar_tensor_tensor(out=PT[:], in0=XTs[0][:], scalar=-1.0, in1=bc_mid(idf), op0=ALU.mult, op1=ALU.add), r=['XT0', 'idf'], w=['PT'], n=512)
                  for c in range(4):
                      P.add('pe', (lambda c: lambda e: e.transpose(out=ps[5][:, c * 128:(c + 1) * 128], in_=XTs[0][:, c, :], identity=idf[:]))(c), r=['XT0', 'idf'], w=['ps5'], f32=True)
                  P.add('act', lambda e: e.activation(out=Xs[0][:].rearrange("p c f -> p (c f)"), in_=ps[5][:, :], func=AF.Copy), r=['ps5'], w=['X0'], n=512)
                  P.stage('kk')
                  cur = 0
                  for lev in range(1, 7):
                      nxt = 1 - cur
                      Xc, XTc, Xn, XTn = Xs[cur], XTs[cur], Xs[nxt], XTs[nxt]
                      rc = ['X%d' % cur, 'XT%d' % cur]
                      for c in range(4):
                          P.add('pe', (lambda c, Xc, XTc: lambda e: e.matmul(ps[4][:, c * 128:(c + 1) * 128], lhsT=XTc[:, c, :], rhs=Xc[:, c, :], start=True, stop=True))(c, Xc, XTc), r=rc, w=['ps4'], f32=True)
                      P.add('act', (lambda Xn: lambda e: e.activation(out=Xn[:].rearrange("p c f -> p (c f)"), in_=ps[4][:, :], func=AF.Copy))(Xn), r=['ps4'], w=['X%d' % nxt], n=512)
                      if lev < 6:
                          for c in range(4):
                              P.add('pe', (lambda c, Xc, XTc: lambda e: e.matmul(ps[5][:, c * 128:(c + 1) * 128], lhsT=Xc[:, c, :], rhs=XTc[:, c, :], start=True, stop=True))(c, Xc, XTc), r=rc, w=['ps5'], f32=True)
                          P.add('dve', (lambda XTn: lambda e: e.tensor_copy(out=XTn[:].rearrange("p c f -> p (c f)"), in_=ps[5][:, :]))(XTn), r=['ps5'], w=['XT%d' % nxt], n=512)
                      for c in range(4):
                          P.add('pe', (lambda c, Xn: lambda e: e.matmul(ps[6][:, c * 128:(c + 1) * 128], lhsT=Xn[:, c, :], rhs=PT[:, c, :], start=True, stop=True))(c, Xn), r=['X%d' % nxt, 'PT'], w=['ps6'], f32=True)
                      P.add('dve', lambda e: e.tensor_tensor(out=PT[:], in0=PT[:], in1=c4(ps[6][:, :]), op=ALU.add), r=['ps6', 'PT'], w=['PT'], n=512)
                      cur = nxt
                  P.add('act', lambda e: e.activation(out=TTb[:], in_=PT[:], func=AF.Copy), r=['PT'], w=['TTb'], n=512)
                  P.stage('neu')
                  for c in range(4):
                      P.add('pe', (lambda c: lambda e: e.transpose(out=pb[0][:, c * 256:c * 256 + 128], in_=knT[:, c * 128:(c + 1) * 128], identity=idb[:]))(c), r=['knT%d' % sa, 'idb'], w=['pb0'])
                      P.add('pe', (lambda c: lambda e: e.transpose(out=pb[0][:, c * 256 + 128:c * 256 + 256], in_=raw[2][:, c * 128:(c + 1) * 128], identity=idb[:]))(c), r=['raw2_%d' % sa, 'idb'], w=['pb0'])
                  P.add('act', lambda e: e.activation(out=kvt[:].rearrange("p c a f -> p (c a f)"), in_=pb[0][:, :], func=AF.Copy), r=['pb0'], w=['kvt'], n=1024)
                  P.add('pool', (lambda hs: lambda e: e.tensor_tensor(out=hs['kd'][:], in0=kvt[:, :, 0, :], in1=bc_last(sc[:, 8:12]), op=ALU.mult))(hs), r=['kvt', 'sc%d' % sa], w=[hb + 'kd'], n=512)
                  P.add('dve', lambda e: e.tensor_tensor(out=kbg[:], in0=kvt[:, :, 0, :], in1=bc_last(sc[:, 12:16]), op=ALU.mult), r=['kvt', 'sc%d' % sa], w=['kbg'], n=512)
                  P.add('pool', lambda e: e.tensor_tensor(out=vb[:], in0=kvt[:, :, 1, :], in1=bc_last(gb[:, 4:8]), op=ALU.mult), r=['kvt', 'gb%d' % sa], w=['vb'], n=512)
                  for c in range(4):
                      P.add('pe', (lambda c: lambda e: e.matmul(ps[4][:, c * 128:(c + 1) * 128], lhsT=TTb[:, c, :], rhs=vb[:, c, :], start=True, stop=True))(c), r=['TTb', 'vb'], w=['ps4'])
                      P.add('pe', (lambda c: lambda e: e.matmul(ps[5][:, c * 128:(c + 1) * 128], lhsT=kbg[:, c, :], rhs=TTb[:, c, :], start=True, stop=True))(c), r=['TTb', 'kbg'], w=['ps5'])
                  P.add('act', (lambda hs: lambda e: e.activation(out=hs['val'][:].rearrange("p c f -> p (c f)"), in_=ps[4][:, :], func=AF.Copy))(hs), r=['ps4'], w=[hb + 'val'], n=512)
                  P.add('act', (lambda hs: lambda e: e.activation(out=hs['kcdT'][:].rearrange("p c f -> p (c f)"), in_=ps[5][:, :], func=AF.Copy))(hs), r=['ps5'], w=[hb + 'kcdT'], n=512)
                  P.stage('tm')
                  for c in range(4):
                      vn_ = vnews[c % 2]; vnn = 'vnew%d' % (c % 2)
                      P.add('pe', (lambda c, hs: lambda e: e.matmul(ps[6][:, 0:128], lhsT=hs['kcdT'][:, c, :], rhs=Sb[:], start=True, stop=True))(c, hs), r=[hb + 'kcdT', 'Sb'], w=['ps6'])
                      P.add('dve', (lambda c, hs, vn_: lambda e: e.tensor_tensor(out=vn_[:], in0=hs['val'][:, c, :], in1=ps[6][:, 0:128], op=ALU.subtract))(c, hs, vn_), r=[hb + 'val', 'ps6'], w=[vnn])
                      P.add('pe', (lambda c, hs: lambda e: e.matmul(ps[4][:, c * 128:(c + 1) * 128], lhsT=hs['qgT'][:, c, :], rhs=Sb[:], start=True, stop=False))(c, hs), r=[hb + 'qgT', 'Sb'], w=['ps4'])
                      P.add('pe', (lambda c, hs, vn_: lambda e: e.matmul(ps[4][:, c * 128:(c + 1) * 128], lhsT=hs['aiT'][:, c, :], rhs=vn_[:], start=False, stop=True))(c, hs, vn_), r=[hb + 'aiT', vnn], w=['ps4'])
                      P.add('pe', (lambda c, hs, vn_: lambda e: e.matmul(ps[6][:, 128:256], lhsT=hs['kd'][:, c, :], rhs=vn_[:], start=True, stop=True))(c, hs, vn_), r=[hb + 'kd', vnn], w=['ps6'])
                      P.add('dve', (lambda c: lambda e: e.scalar_tensor_tensor(out=Sb[:], in0=Sf[:], scalar=sc[:, 4 + c:5 + c], in1=ps[6][:, 128:256], op0=ALU.mult, op1=ALU.add))(c), r=['Sf', 'sc%d' % sa, 'ps6'], w=['Sb'])
                      P.add('dve', (lambda c: lambda e: e.scalar_tensor_tensor(out=Sf[:], in0=Sf[:], scalar=sc[:, 4 + c:5 + c], in1=ps[6][:, 128:256], op0=ALU.mult, op1=ALU.add))(c), r=['Sf', 'sc%d' % sa, 'ps6', 'Sb'], w=['Sf'])
                  P.stage('chain')
                  P.add('act', lambda e: e.activation(out=osq[:].rearrange("p c f -> p (c f)"), in_=ps[4][:, :], func=AF.Square), r=['ps4'], w=['osq'], n=512)
                  P.add('dve', lambda e: e.tensor_reduce(out=ssq[:, 0:4], in_=osq[:], axis=AX.X, op=ALU.add), r=['osq'], w=['ssq'], n=512)
                  P.add('act', lambda e: e.activation(out=ssq[:, 4:8], in_=ssq[:, 0:4], func=AF.Sqrt, scale=1.0 / 128, bias=epsc[:, 0:1]), r=['ssq', 'epsc'], w=['ssq'])
                  P.add('dve', lambda e: e.reciprocal(out=ssq[:, 4:8], in_=ssq[:, 4:8]), r=['ssq'], w=['ssq'])
                  P.add('dve', lambda e: e.tensor_tensor(out=osq[:], in0=c4(ps[4][:, :]), in1=bc_last(ssq[:, 4:8]), op=ALU.mult), r=['ps4', 'ssq', 'osq'], w=['osq'], n=512)
                  P.add('pool', (lambda hs: lambda e: e.tensor_tensor(out=hs['ost'][:], in0=osq[:], in1=bc_mid(dngs), op=ALU.mult))(hs), r=['osq', 'dngs'], w=[hb + 'ost'], n=512)
                  P.add('sp', (lambda hs, b: lambda e: e.dma_start(out=ib.ap()[b * 512:(b + 1) * 512, :].rearrange("(c p) d -> p c d", p=128), in_=hs['ost'][:]))(hs, b), r=[hb + 'ost'], w=['ib'], dma='o%d' % bp)
              def xload(P_, b):
                  sl = b % 2
                  P_.add('pool', (lambda b, xb: lambda e: e.dma_start(out=xb[:], in_=xT.rearrange("(k p) t -> p k t", p=128)[:, :, b * 512:(b + 1) * 512]))(b, xblk[sl]),
                         w=['xblk%d' % sl], dma='x%d' % sl, n=4194304)
              xload(P, 0)
              for step in range(nblk + 1):
                  ra, rb = Rec(), Rec()
                  if step + 1 < nblk: xload(ra, step + 1)
                  if step == max(nblk - 3, 0) and 'A1' in phases:
                      ra.add('pool', lambda e: e.dma_start(out=xg[:], in_=xTm.rearrange("(k p) t -> p k t", p=128)), w=['xg_pre'], dma='xg', force=True)
                  if step < nblk: streamA(ra, step)
                  if step >= 1: streamB(rb, step - 1)
                  la, lb = len(ra.items), len(rb.items)
                  ia = ib_ = 0
                  while ia < la or ib_ < lb:
                      if ib_ >= lb or (ia < la and ia * lb <= ib_ * la):
                          a_, k_ = ra.items[ia]; ia += 1
                      else:
                          a_, k_ = rb.items[ib_]; ib_ += 1
                      P.add(*a_, **k_)
              if DEBUG:
                  P.add('sp', lambda e: e.dma_start(out=dbg_o[0:nblk * 512, :], in_=ib.ap()[0:nblk * 512, :]), r=['ib'], w=['dbgo'], dma='dbg')
                  P.add('sp', lambda e: None, r=['dbgo'])
              P.add('sp', lambda e: None, r=['ib'])
              P.emit(st, top, extra_sems={'xg': pre_sems['xg']})
        nc.all_engine_barrier()

        with contextlib.ExitStack() as st:
          if 'A1' in phases:
              ps = [st.enter_context(nc.psum_tensor("psB%d" % i, [128, 512], F32)) for i in range(6)]
              pb = [st.enter_context(nc.psum_tensor("pbB%d" % i, [128, 1024], BF16)) for i in range(2)]
              P = Prog(nc)
              T = lambda name, shp, dt: TT(st, name, shp, dt)
              lngs = T("lngs1", [128, 16], F32); mngs = T("mngs", [128, 16], F32)
              wblk = [T("wblk%d" % i, [128, 16, 512], BF16) for i in range(2)]
              xt = [T("xt%d" % i, [128, D], F32) for i in range(1)]
              rsm = T("rsm", [128, 24], F32)
              gu = T("gu", [128, NT, 512], BF16); vn = T("vn", [128, NT, 512], BF16); sz = T("sz", [128, NT, 512], BF16)
              scz = T("scz", [128, NT, 512], BF16); cqT = T("cqT", [128, 4, TS], BF16)
              junk = gu[:, 0:4, :].rearrange("p a c -> p (a c)")
              uu = [T("uu%d" % i, [128, 512], F32) for i in range(2)]; t1 = [T("t1_%d" % i, [128, 512], F32) for i in range(2)]
              gv = [T("gv%d" % i, [128, 512], F32) for i in range(2)]
              lsts = [T("lst%d" % i, [128, 16], F32) for i in range(2)]
              glgs = T("glgs", [128, 512], F32); glbs = T("glbs", [128, 512], F32)
              wsf = T("wsf", [128, 4, 128], F32); wsb = T("wsb", [128, 4, 128], BF16); bss = T("bss", [128, 4], F32)
              ga = [T("ga%d" % i, [128, 512], F32) for i in range(2)]
              memTb = T("memTb", [128, 16, 256], BF16)
              rmm = T("rmm", [128, 8], F32)
              mk = T("mk", [128, 2, 512], BF16); mv = T("mv", [128, 2, 512], BF16); mkT = T("mkT", [128, 4, 256], BF16)
              pex = [T("pex%d" % i, [128, 256], BF16) for i in range(2)]; pT = [T("pT%d" % i, [128, 2, 128], BF16) for i in range(2)]
              sm = [T("sm%d" % i, [128, 4], F32) for i in range(2)]

              nblk = [0]
              def load_w(src_ap, c0):
                  i = nblk[0] % 2; nblk[0] += 1
                  wb = wblk[i]; wn = 'wblk%d' % i
                  P.add('pool', (lambda wb, c0: lambda e: e.dma_start(out=wb[:], in_=src_ap.rearrange("(k p) c -> p k c", p=128)[:, :, c0:c0 + 512]))(wb, c0), w=[wn], dma=wn)
                  return wb, wn
              PRE = 'A2' in phases
              if not PRE:
                  P.add('pool', lambda e: e.dma_start(out=xg[:], in_=xTm.rearrange("(k p) t -> p k t", p=128)), w=['xg'], dma='xg')
              nxt_w = load_w(wtok, 0)
              P.add('sp', lambda e: e.dma_start(out=lngs[:], in_=lng), w=['lngs'], dma='u6')
              P.add('sp', lambda e: e.dma_start(out=mngs[:], in_=mng), w=['mngs'], dma='u7')
              P.add('sp', lambda e: e.dma_start(out=glgs[:], in_=glg), w=['glgs'], dma='u8')
              P.add('sp', lambda e: e.dma_start(out=glbs[:], in_=glb), w=['glbs'], dma='u9')
              P.add('sp', lambda e: e.dma_start(out=wsf[:].rearrange("p g t -> p (g t)"), in_=wsT), w=['wsf'], dma='u10')
              P.add('sp', lambda e: e.dma_start(out=bss[:], in_=bsT), w=['bss'], dma='u11')
              for k in range(16):
                  def prew(e, k):
                      if PRE and k < 2: e.wait_ge(pre_sems['xg'], 16)
                  if k % 2 == 0:
                      P.add('dve', (lambda k: lambda e: (prew(e, k), e.tensor_scalar(out=xg[:, k, :], in0=xg[:, k, :], scalar1=lngs[:, k:k + 1], scalar2=None, op0=ALU.mult))[1])(k), r=['xg', 'lngs'], w=['xg%d' % k])
                  else:
                      P.add('act', (lambda k: lambda e: (prew(e, k), e.activation(out=xg[:, k, :], in_=xg[:, k, :], func=AF.Copy, scale=lngs[:, k:k + 1]))[1])(k), r=['xg', 'lngs'], w=['xg%d' % k])
              for t in range(NT):
                  xs = xt[0]; xn = 'xt0'
                  P.add('sp', (lambda t, xs: lambda e: e.dma_start(out=xs[:], in_=xtok[t * 128:(t + 1) * 128, :]))(t, xs), w=[xn], dma=xn)
                  P.add('act', (lambda t, xs: lambda e: e.activation(out=junk, in_=xs[:], func=AF.Square, accum_out=rsm[:, t:t + 1]))(t, xs), r=[xn], w=['gu', 'rsm'])
              P.add('act', lambda e: e.activation(out=rsm[:, 8:16], in_=rsm[:, 0:8], func=AF.Sqrt, scale=1.0 / D, bias=epsc[:, 0:1]), r=['rsm', 'epsc'], w=['rsm'])
              P.add('dve', lambda e: e.reciprocal(out=rsm[:, 8:16], in_=rsm[:, 8:16]), r=['rsm'], w=['rsm'])
              P.add('dve', lambda e: e.tensor_scalar(out=rsm[:, 16:24], in0=rsm[:, 8:16], scalar1=128.0 ** -0.5, scalar2=None, op0=ALU.mult), r=['rsm'], w=['rsm'])
              for a in range(2):
                  P.add('sp', (lambda a: lambda e: e.dma_start(out=xt[0][:], in_=memd[a * 128:(a + 1) * 128, :]))(a), w=['xt0'], dma='xt0')
                  P.add('act', (lambda a: lambda e: e.activation(out=junk, in_=xt[0][:], func=AF.Square, accum_out=rmm[:, a:a + 1]))(a), r=['xt0'], w=['gu', 'rmm'])
              P.add('act', lambda e: e.activation(out=rmm[:, 2:4], in_=rmm[:, 0:2], func=AF.Sqrt, scale=1.0 / D, bias=epsc[:, 0:1]), r=['rmm', 'epsc'], w=['rmm'])
              P.add('dve', lambda e: e.reciprocal(out=rmm[:, 2:4], in_=rmm[:, 2:4]), r=['rmm'], w=['rmm'])

              pcount = [0]
              def nextps():
                  i = pcount[0] % 4; pcount[0] += 1
                  return ps[i], 'ps%d' % i

              def gelu_to(pt, pn, t, dst_fn, dst_name, i2):
                  u_, t_ = uu[i2], t1[i2]
                  P.add('act', (lambda: lambda e: e.activation(out=u_[:], in_=pt[:, :], func=AF.Copy, scale=rsm[:, 8 + t:9 + t]))(), r=[pn, 'rsm'], w=['uu%d' % i2])
                  P.add('act', (lambda: lambda e: e.activation(out=t_[:], in_=u_[:], func=AF.Square))(), r=['uu%d' % i2], w=['t1_%d' % i2])
                  P.add('dve', (lambda: lambda e: e.tensor_scalar(out=t_[:], in0=t_[:], scalar1=0.044715, scalar2=1.0, op0=ALU.mult, op1=ALU.add))(), r=['t1_%d' % i2], w=['t1_%d' % i2])
                  P.add('pool', (lambda: lambda e: e.tensor_tensor(out=t_[:], in0=t_[:], in1=u_[:], op=ALU.mult))(), r=['t1_%d' % i2, 'uu%d' % i2], w=['t1_%d' % i2])
                  P.add('act', (lambda: lambda e: e.activation(out=t_[:], in_=t_[:], func=AF.Sigmoid, scale=1.5957691216057308))(), r=['t1_%d' % i2], w=['t1_%d' % i2])
                  P.add('dve', (lambda: lambda e: e.tensor_tensor(out=dst_fn(), in0=u_[:], in1=t_[:], op=ALU.mult))(), r=['t1_%d' % i2, 'uu%d' % i2], w=[dst_name])

              Pmain = P
              for cb in range(7):
                  wb, wn = nxt_w
                  if cb + 1 < 7: nxt_w = load_w(wtok, (cb + 1) * 512)
                  else: nxt_w = load_w(wkv, 0)
                  recs = [Rec(), Rec()]
                  if cb == 5:
                      for hh in range(4):
                          for half in range(2):
                              P = recs[half]
                              pt, pn = nextps()
                              for k in range(16):
                                  P.add('pe', (lambda pt, k, hh, half, wb: lambda e: e.matmul(pt[:, :], lhsT=wb[:, k, hh * 128:(hh + 1) * 128], rhs=xg[:, k, half * 512:(half + 1) * 512], start=(k == 0), stop=(k == 15)))(pt, k, hh, half, wb),
                                        r=[wn, 'xg%d' % k], w=[pn])
                              P.add('act', (lambda pt, hh, half: lambda e: e.activation(out=cqT[:, hh, half * 512:(half + 1) * 512], in_=pt[:, :], func=AF.Copy))(pt, hh, half), r=[pn], w=['cqT'])
                      P = Pmain; merge_into(P, recs)
                      continue
                  for t in range(NT):
                      P = recs[t % 2]
                      pt, pn = nextps()
                      def pewait(e, k, t, cb=cb):
                          pass
                      for k in range(16):
                          P.add('pe', (lambda pt, k, t, wb: lambda e: (pewait(e, k, t), e.matmul(pt[:, :], lhsT=xg[:, k, t * 128:(t + 1) * 128], rhs=wb[:, k, :], start=(k == 0), stop=(k == 15)))[1])(pt, k, t, wb),
                                r=[wn, 'xg%d' % k], w=[pn])
                      i2 = t % 2
                      if cb == 0:
                          gelu_to(pt, pn, t, (lambda t: lambda: gu[:, t, :])(t), 'gu', i2)
                      elif cb == 1:
                          g_ = gv[i2]; gn = 'gv%d' % i2; lst = lsts[i2]; ln_ = 'lst%d' % i2
                          gelu_to(pt, pn, t, (lambda g_: lambda: g_[:])(g_), gn, i2)
                          P.add('act', (lambda g_: lambda e, lst=lst, tj=t1[i2]: e.activation(out=tj[:], in_=g_[:], func=AF.Copy, accum_out=lst[:, 0:1]))(g_), r=[gn], w=['t1_%d' % i2, ln_])
                          P.add('act', (lambda g_: lambda e, lst=lst, tj=t1[i2]: e.activation(out=tj[:], in_=g_[:], func=AF.Square, accum_out=lst[:, 1:2]))(g_), r=[gn, ln_], w=['t1_%d' % i2, ln_])
                          P.add('dve', lambda e, lst=lst: e.tensor_scalar(out=lst[:, 2:4], in0=lst[:, 0:2], scalar1=1.0 / 512, scalar2=None, op0=ALU.mult), r=[ln_], w=[ln_])
                          P.add('dve', lambda e, lst=lst: e.tensor_tensor(out=lst[:, 4:5], in0=lst[:, 2:3], in1=lst[:, 2:3], op=ALU.mult), r=[ln_], w=[ln_])
                          P.add('dve', lambda e, lst=lst: e.tensor_tensor(out=lst[:, 5:6], in0=lst[:, 3:4], in1=lst[:, 4:5], op=ALU.subtract), r=[ln_], w=[ln_])
                          P.add('act', lambda e, lst=lst: e.activation(out=lst[:, 6:7], in_=lst[:, 5:6], func=AF.Sqrt, scale=1.0, bias=epsc[:, 0:1]), r=[ln_, 'epsc'], w=[ln_])
                          P.add('dve', lambda e, lst=lst: e.reciprocal(out=lst[:, 6:7], in_=lst[:, 6:7]), r=[ln_], w=[ln_])
                          P.add('dve', (lambda g_: lambda e, lst=lst: e.tensor_scalar(out=g_[:], in0=g_[:], scalar1=lst[:, 2:3], scalar2=lst[:, 6:7], op0=ALU.subtract, op1=ALU.mult))(g_), r=[gn, ln_], w=[gn])
                          P.add('pool', (lambda g_: lambda e, lst=lst: e.tensor_tensor(out=g_[:], in0=g_[:], in1=glgs[:], op=ALU.mult))(g_), r=[gn, 'glgs'], w=[gn])
                          P.add('pool', (lambda g_, t: lambda e, lst=lst: e.tensor_tensor(out=vn[:, t, :], in0=g_[:], in1=glbs[:], op=ALU.add))(g_, t), r=[gn, 'glbs'], w=['vn'])
                      else:
                          if cb == 2: dst = (lambda t: lambda: sz[:, t, :])(t); dn = 'sz'
                          elif cb == 3: dst = (lambda t: lambda: sdz[:, t, 0:512])(t); dn = 'sdz'
                          elif cb == 4: dst = (lambda t: lambda: sdz[:, t, 512:1024])(t); dn = 'sdz'
                          else: dst = (lambda t: lambda: scz[:, t, :])(t); dn = 'scz'
                          P.add('act', (lambda pt, t, dst: lambda e: e.activation(out=dst(), in_=pt[:, :], func=AF.Silu, scale=rsm[:, 8 + t:9 + t]))(pt, t, dst), r=[pn, 'rsm'], w=[dn])
                  P = Pmain; merge_into(P, recs)
              P.add('pool', lambda e: e.dma_start(out=memTb[:], in_=memT.rearrange("(k p) m -> p k m", p=128)), w=['memTb'], dma='u13')
              for k in range(16):
                  P.add('act', (lambda k: lambda e: e.activation(out=memTb[:, k, :], in_=memTb[:, k, :], func=AF.Copy, scale=mngs[:, k:k + 1]))(k), r=['memTb', 'mngs'], w=['memTb'])
              for g in range(4):
                  P.add('pool', (lambda g: lambda e: e.affine_select(out=wsf[:, g, :], in_=wsf[:, g, :], pattern=[[1, 128]], compare_op=ALU.is_ge, fill=0.0, base=0, channel_multiplier=-1))(g), r=['wsf'], w=['wsf'])
              P.add('dve', lambda e: e.tensor_copy(out=wsb[:], in_=wsf[:]), r=['wsf'], w=['wsb'])
              recs = [Rec(), Rec()]
              for t in range(NT):
                  P = recs[t % 2]
                  pt, pn = nextps()
                  for g in range(4):
                      P.add('pe', (lambda pt, g, t: lambda e: e.matmul(pt[:, g * 128:(g + 1) * 128], lhsT=wsb[:, g, :], rhs=vn[:, t, g * 128:(g + 1) * 128], start=True, stop=True))(pt, g, t),
                            r=['wsb', 'vn'], w=[pn])
                  g_ = ga[t % 2]; gn = 'ga%d' % (t % 2)
                  for g in range(4):
                      P.add('dve', (lambda pt, g, t, g_: lambda e: e.scalar_tensor_tensor(out=g_[:, g * 128:(g + 1) * 128], in0=pt[:, g * 128:(g + 1) * 128], scalar=bss[:, g:g + 1],
                                                                                    in1=gu[:, t, g * 128:(g + 1) * 128], op0=ALU.add, op1=ALU.mult))(pt, g, t, g_), r=[pn, 'bss', 'gu', gn], w=[gn])
                  P.add('pool', (lambda t, g_: lambda e: e.tensor_tensor(out=out_a[:, t, :], in0=g_[:], in1=sz[:, t, :], op=ALU.mult))(t, g_), r=[gn, 'sz'], w=['out_a'])
              P = Pmain; merge_into(P, recs)
              for half in range(2):
                  wb, wn = nxt_w
                  if half == 0:
                      nxt_w = load_w(wkv, 512)
                      if 'B' in phases and 'A2' in phases:
                          P.add('pool', lambda e: e.collective_compute("AllGather", ALU.bypass, replica_groups=[list(range(NCORE))], ins=[ib.ap().opt()], outs=[ob.ap().opt()]),
                                w=['ob'], dma='cc', force=True)
                  for a in range(2):
                      pt, pn = nextps()
                      for k in range(16):
                          P.add('pe', (lambda pt, k, a, wb: lambda e: e.matmul(pt[:, :], lhsT=memTb[:, k, a * 128:(a + 1) * 128], rhs=wb[:, k, :], start=(k == 0), stop=(k == 15)))(pt, k, a, wb),
                                r=[wn, 'memTb'], w=[pn])
                      dst = mk if half == 0 else mv
                      P.add('act', (lambda pt, a, dst: lambda e: e.activation(out=dst[:, a, :], in_=pt[:, :], func=AF.Copy, scale=rmm[:, 2 + a:3 + a]))(pt, a, dst), r=[pn, 'rmm'], w=['mk' if half == 0 else 'mv'])
              for hh in range(4):
                  for a in range(2):
                      P.add('pe', (lambda hh, a: lambda e: e.transpose(out=pb[0][:, 0:128], in_=mk[:, a, hh * 128:(hh + 1) * 128], identity=idb[:]))(hh, a), r=['mk', 'idb'], w=['pb0'])
                      P.add('act', (lambda hh, a: lambda e: e.activation(out=mkT[:, hh, a * 128:(a + 1) * 128], in_=pb[0][:, 0:128], func=AF.Copy))(hh, a), r=['pb0'], w=['mkT'])
              it = 0
              recs = [Rec(), Rec()]
              for t in range(NT):
                  for hh in range(4):
                      i2 = it % 2; it += 1
                      P = recs[i2]
                      pt, pn = nextps()
                      P.add('pe', (lambda pt, t, hh: lambda e: e.matmul(pt[:, 0:256], lhsT=cqT[:, hh, t * 128:(t + 1) * 128], rhs=mkT[:, hh, :], start=True, stop=True))(pt, t, hh), r=['cqT', 'mkT'], w=[pn])
                      s_ = sm[i2]; sn = 'sm%d' % i2; pe_ = pex[i2]; pen = 'pex%d' % i2; pT_ = pT[i2]; pTn = 'pT%d' % i2
                      P.add('dve', (lambda pt, s_: lambda e: e.tensor_reduce(out=s_[:, 0:1], in_=pt[:, 0:256], axis=AX.X, op=ALU.max))(pt, s_), r=[pn], w=[sn])
                      P.add('dve', (lambda s_, t: lambda e: e.scalar_tensor_tensor(out=s_[:, 1:2], in0=s_[:, 0:1], scalar=-1.0, in1=rsm[:, 16 + t:17 + t], op0=ALU.mult, op1=ALU.mult))(s_, t), r=[sn, 'rsm'], w=[sn])
                      P.add('act', (lambda pt, s_, t, pe_: lambda e: e.activation(out=pe_[:], in_=pt[:, 0:256], func=AF.Exp, bias=s_[:, 1:2], scale=rsm[:, 16 + t:17 + t], accum_out=s_[:, 2:3]))(pt, s_, t, pe_),
                            r=[pn, sn, 'rsm'], w=[pen, sn])
                      P.add('dve', (lambda s_: lambda e: e.reciprocal(out=s_[:, 3:4], in_=s_[:, 2:3]))(s_), r=[sn], w=[sn])
                      pbt = pb[i2]; pbn = 'pb%d' % i2
                      for a in range(2):
                          P.add('pe', (lambda a, pe_, pbt: lambda e: e.transpose(out=pbt[:, a * 128:(a + 1) * 128], in_=pe_[:, a * 128:(a + 1) * 128], identity=idb[:]))(a, pe_, pbt), r=[pen, 'idb'], w=[pbn])
                      P.add('act', (lambda pT_, pbt: lambda e: e.activation(out=pT_[:].rearrange("p a t -> p (a t)"), in_=pbt[:, 0:256], func=AF.Copy))(pT_, pbt), r=[pbn], w=[pTn])
                      pt2, pn2 = nextps()
                      for a in range(2):
                          P.add('pe', (lambda a, pT_, pt2, hh: lambda e: e.matmul(pt2[:, 0:128], lhsT=pT_[:, a, :], rhs=mv[:, a, hh * 128:(hh + 1) * 128], start=(a == 0), stop=(a == 1)))(a, pT_, pt2, hh),
                                r=[pTn, 'mv'], w=[pn2])
                      P.add('dve', (lambda pt2, s_, t, hh: lambda e: e.scalar_tensor_tensor(out=out_c[:, t, hh * 128:(hh + 1) * 128], in0=pt2[:, 0:128], scalar=s_[:, 3:4], in1=scz[:, t, hh * 128:(hh + 1) * 128],
                                                                                        op0=ALU.mult, op1=ALU.mult))(pt2, s_, t, hh), r=[pn2, sn, 'scz'], w=['out_c'])
              P = Pmain; merge_into(P, recs)
              if DEBUG:
                  P.add('sp', lambda e: e.dma_start(out=dbg_a, in_=out_a[:].rearrange("p a c -> p (a c)")), r=['out_a'], w=['dbga'], dma='dbga')
                  P.add('sp', lambda e: e.dma_start(out=dbg_c, in_=out_c[:].rearrange("p a c -> p (a c)")), r=['out_c'], w=['dbgc'], dma='dbgc')
                  P.add('sp', lambda e: None, r=['dbga', 'dbgc'])
              P.emit(st, top, extra_sems={'cc': cc_sem})
        nc.all_engine_barrier()
        mid.close()

        with contextlib.ExitStack() as st:
          if 'B' in phases:
              ps = [st.enter_context(nc.psum_tensor("psC%d" % i, [128, 512], F32)) for i in range(6)]
              pb = [st.enter_context(nc.psum_tensor("pbC%d" % i, [128, 1024], BF16)) for i in range(2)]
              P = Prog(nc)
              T = lambda name, shp, dt: TT(st, name, shp, dt)
              wblk = [T("wblkB%d" % i, [128, 16, 512], BF16) for i in range(2)]
              og = [T("og%d" % i, [128, 8, 128], BF16) for i in range(2)]
              gt = [T("gt%d" % i, [128, 1024], BF16) for i in range(2)]
              mT = T("mT", [128, NT, 16, 128], BF16)
              yp = T("yp", [128, NT, D], F32)
              fgs = T("fgs", [128, D], F32); junk = T("junkB", [128, D], BF16)
              fs = T("fs", [128, 16], F32)
              P.add('sp', lambda e: e.dma_start(out=fgs[:], in_=fgd), w=['fgs'], dma='u14')
              def og_dma(t, o_):
                  def fn(e):
                      if t == 0:
                          e.wait_ge(cc_sem, 1)
                      return e.dma_start(out=o_[:], in_=ob.ap().rearrange("(r s) c -> s r c", r=NCORE)[bass.ds(offv[0] * TS + t * 128, 128), :, :])
                  return fn
              def wload(cb):
                  wb = wblk[cb % 2]; wn = 'wblkB%d' % (cb % 2)
                  P.add('pool', (lambda wb, cb: lambda e: e.dma_start(out=wb[:], in_=wout.rearrange("(k p) c -> p k c", p=128)[:, :, cb * 512:(cb + 1) * 512]))(wb, cb), w=[wn], dma=wn)
              wload(0); wload(1)
              Pmain = P
              recs = [Rec(), Rec()]
              for t in range(NT):
                  o_ = og[t % 2]; on = 'og%d' % (t % 2); g_ = gt[t % 2]; gn = 'gt%d' % (t % 2)
                  P = recs[t % 2]
                  P.add('sp', og_dma(t, o_), w=[on], dma=on)
                  P.add('dve', (lambda t, o_, g_: lambda e: e.tensor_tensor(out=g_[:], in0=o_[:].rearrange("p r c -> p (r c)"), in1=sdz[:, t, :], op=ALU.mult))(t, o_, g_), r=[on, 'sdz'], w=[gn])
                  for j in range(16):
                      if j < 4: src = (lambda t, j: lambda: out_a[:, t, j * 128:(j + 1) * 128])(t, j); rn_ = 'out_a'
                      elif j < 12: src = (lambda g_, j: lambda: g_[:, (j - 4) * 128:(j - 3) * 128])(g_, j); rn_ = gn
                      else: src = (lambda t, j: lambda: out_c[:, t, (j - 12) * 128:(j - 11) * 128])(t, j); rn_ = 'out_c'
                      pbt = pb[t % 2]; pbn = 'pb%d' % (t % 2)
                      pbt_cols = slice((j % 4) * 128, (j % 4) * 128 + 128)
                      P.add('pe', (lambda src, pbt, j: lambda e: e.transpose(out=pbt[:, (j % 8) * 128:(j % 8) * 128 + 128], in_=src(), identity=idb[:]))(src, pbt, j), r=[rn_, 'idb'], w=[pbn])
                      if j % 8 == 7:
                          j0 = j - 7
                          P.add('act', (lambda t, j0, pbt: lambda e: e.activation(out=mT[:, t, j0:j0 + 8, :].rearrange("p a c -> p (a c)"), in_=pbt[:, 0:1024], func=AF.Copy))(t, j0, pbt), r=[pbn], w=['mT%d' % t], n=1024)
              P = Pmain; merge_into(P, recs)
              P.add('sp', lambda e: e.dma_start(out=yp[:], in_=xtok.rearrange("(a p) c -> p a c", p=128)), w=['yp%d' % t for t in range(NT)], dma='u15')
              pc = 0
              for cb in range(4):
                  wb = wblk[cb % 2]; wn = 'wblkB%d' % (cb % 2)
                  for t in range(NT):
                      pt = ps[pc % 4]; pn = 'ps%d' % (pc % 4); pc += 1
                      for k in range(16):
                          P.add('pe', (lambda pt, k, t, wb: lambda e: e.matmul(pt[:, :], lhsT=mT[:, t, k, :], rhs=wb[:, k, :], start=(k == 0), stop=(k == 15)))(pt, k, t, wb), r=[wn, 'mT%d' % t], w=[pn])
                      P.add('dve', (lambda pt, t, cb: lambda e: e.tensor_tensor(out=yp[:, t, cb * 512:(cb + 1) * 512], in0=yp[:, t, cb * 512:(cb + 1) * 512], in1=pt[:, :], op=ALU.add))(pt, t, cb), r=[pn, 'yp%d' % t], w=['yp%d' % t])
                  if cb + 2 < 4: wload(cb + 2)
              for t in range(NT):
                  P.add('act', (lambda t: lambda e: e.activation(out=junk[:], in_=yp[:, t, :], func=AF.Square, accum_out=fs[:, t:t + 1]))(t), r=['yp%d' % t], w=['junkB', 'fs'])
              P.add('act', lambda e: e.activation(out=fs[:, 8:16], in_=fs[:, 0:8], func=AF.Sqrt, scale=1.0 / D, bias=epsc[:, 0:1]), r=['fs', 'epsc'], w=['fs'])
              P.add('dve', lambda e: e.reciprocal(out=fs[:, 8:16], in_=fs[:, 8:16]), r=['fs'], w=['fs'])
              for t in range(NT):
                  eng = 'dve'
                  P.add(eng, (lambda t: lambda e: e.scalar_tensor_tensor(out=yp[:, t, :], in0=yp[:, t, :], scalar=fs[:, 8 + t:9 + t], in1=fgs[:], op0=ALU.mult, op1=ALU.mult))(t), r=['yp%d' % t, 'fs', 'fgs'], w=['yp%d' % t])
                  P.add('sp', (lambda t: lambda e: e.dma_start(out=y[t * 128:(t + 1) * 128, :], in_=yp[:, t, :]))(t), r=['yp%d' % t], w=['y%d' % t], dma='yst%d' % t)
              P.add('sp', lambda e: None, r=['y%d' % t for t in range(NT)])
              P.emit(st, top)
    return nc


SEG = dict(u=0, v=512, z=1024, dq=1536, dk=2560, dv=3584, dz=4608, da=5632, db=5640, cq=5648, cz=6160)


def make_inputs(x, mem, ln_g, w_in, gmlp_ln_g, gmlp_ln_b, gmlp_ws, gmlp_bs, conv_w, dn_a_log,
                dn_dt_bias, dn_norm_g, mem_norm_g, w_mem_kv, w_out, final_g):
    f = lambda a: np.ascontiguousarray(np.asarray(a, dtype=np.float32))
    x0 = f(x)[0]; W = f(w_in)[0]
    xT = np.ascontiguousarray(x0.T)
    wtok = np.ascontiguousarray(np.concatenate([W[:, 0:1536], W[:, 4608:5632], W[:, 5648:6672]], axis=1))
    rep = lambda v: np.ascontiguousarray(np.broadcast_to(f(v).reshape(1, -1), (128, f(v).size)))
    col16 = lambda v: np.ascontiguousarray(f(v).reshape(16, 128).T)
    shared = dict(
        xT=xT, wtok=wtok, lng=col16(ln_g[0]), glg=rep(gmlp_ln_g[0]), glb=rep(gmlp_ln_b[0]),
        wsT=np.ascontiguousarray(f(gmlp_ws)[0].transpose(2, 0, 1).reshape(128, 512)),
        bsT=np.ascontiguousarray(f(gmlp_bs)[0].T), dng=rep(dn_norm_g[0]),
        memT=np.ascontiguousarray(f(mem)[0].T), mem=f(mem)[0], mng=col16(mem_norm_g[0]),
        wkv=f(w_mem_kv)[0], wout=f(w_out)[0], fg=rep(final_g))
    cwf = f(conv_w)[0]
    maps = []
    for i in range(NCORE):
        wd = np.zeros((D, 417), np.float32)
        wd[:, 0:128] = W[:, SEG['dq'] + 128 * i: SEG['dq'] + 128 * (i + 1)]
        wd[:, 128:256] = W[:, SEG['dk'] + 128 * i: SEG['dk'] + 128 * (i + 1)]
        wd[:, 256:384] = W[:, SEG['dv'] + 128 * i: SEG['dv'] + 128 * (i + 1)]
        wd[:, 384] = W[:, SEG['da'] + i]
        wd[:, 416] = W[:, SEG['db'] + i]
        cwi = np.stack([cwf[:, s * 1024 + 128 * i: s * 1024 + 128 * (i + 1)].T for s in range(3)], axis=1)
        hp = np.ascontiguousarray(np.broadcast_to(np.array([[f(dn_a_log)[0, i], f(dn_dt_bias)[0, i]]], np.float32), (128, 2)))
        m = dict(shared)
        m.update(wdn=wd, cw=np.ascontiguousarray(cwi.reshape(128, 12)), hp=hp,
                 xtok=np.ascontiguousarray(x0[i * TS:(i + 1) * TS]), xTm=np.ascontiguousarray(xT[:, i * TS:(i + 1) * TS]),
                 cid=np.array([[i]], np.int32))
        maps.append(m)
    return maps


_NC = None


def kernel(**inputs):
    global _NC
    maps = make_inputs(**inputs)
    if _NC is None:
        _NC = build_nc()
    res = run_bass_kernel_spmd(_NC, maps, core_ids=list(range(NCORE)))
    out = np.concatenate([np.asarray(r["y"], dtype=np.float32) for r in res.results], axis=0)
    kernel.last = res
    return out.reshape(1, S, D)
```
